# Optimizing a Trainium2 kernel written in Bass

```python
import math
import jax
import jax.numpy as jnp
from jax import lax
import numpy as np


D_MODEL = 2048
BATCH = 1
SEQ = 16384
DEPTH = 2

GRID_W = 64
CTX_LEN = 256

CONV_WIDTH = 512
CONV_TAPS = 31
MLA_HEADS = 8
QK_NOPE_DIM = 128
QK_ROPE_DIM = 64
V_HEAD_DIM = 128
Q_LORA_RANK = 512
KV_LORA_RANK = 256
MLA_WIDTH = MLA_HEADS * V_HEAD_DIM
HYENA_WIDTH = 512
HYENA_ORDER = 2
HYENA_SHORT_TAPS = 3
HYENA_EMB_DIM = 33
HYENA_BANDS = (HYENA_EMB_DIM - 1) // 2
HYENA_FILTER_HIDDEN = 64
HYENA_FAST_DECAY_PCT = 0.3
HYENA_SLOW_DECAY_PCT = 1.5
HYENA_DECAY_TARGET = 1e-2
MIX_WIDTH = CONV_WIDTH + MLA_WIDTH + HYENA_WIDTH
D_FF = 4 * D_MODEL
N_MOD = 6
ROPE_BASE = 10000.0
NORM_EPS = 1e-6
ATTN_BLOCK = 128
IN_A = 2 * CONV_WIDTH
IN_B = Q_LORA_RANK + KV_LORA_RANK + QK_ROPE_DIM
IN_C = (HYENA_ORDER + 1) * HYENA_WIDTH
IN_WIDTH = IN_A + IN_B + IN_C

kernel_name = 'hybrid_conv_mla_hyena_dit_trunk'


def rms_norm(x, g):
    xf = x.astype(jnp.float32)
    y = xf * lax.rsqrt(jnp.mean(xf * xf, axis=-1, keepdims=True) + NORM_EPS)
    return (y * g.astype(jnp.float32)).astype(x.dtype)


def layer_norm(x, g, b):
    xf = x.astype(jnp.float32)
    mu = jnp.mean(xf, axis=-1, keepdims=True)
    var = jnp.mean(jnp.square(xf - mu), axis=-1, keepdims=True)
    y = (xf - mu) * lax.rsqrt(var + NORM_EPS) * g.astype(jnp.float32) + b.astype(jnp.float32)
    return y.astype(x.dtype)


def modulate(h, shift, scale):
    return h * (1.0 + scale) + shift


def depthwise_conv(x, w, b):
    taps = w.shape[0]
    pad = (taps - 1) // 2
    y = lax.conv_general_dilated(
        x, w[:, None, :].astype(x.dtype), window_strides=(1,),
        padding=[(pad, taps - 1 - pad)], dimension_numbers=('NWC', 'WIO', 'NWC'),
        feature_group_count=x.shape[-1])
    return y + b.astype(x.dtype)


def axial_rope_angles(rows):
    row = jnp.repeat(jnp.arange(rows, dtype=jnp.float32), GRID_W)
    col = jnp.tile(jnp.arange(GRID_W, dtype=jnp.float32), rows)
    axis_dim = QK_ROPE_DIM // 2
    inv = 1.0 / (ROPE_BASE ** (jnp.arange(0, axis_dim, 2, dtype=jnp.float32) / axis_dim))
    ang = jnp.concatenate([row[:, None] * inv, col[:, None] * inv], axis=-1)
    return jnp.cos(ang), jnp.sin(ang)


def apply_rope(x, cos, sin):
    half = x.shape[-1] // 2
    shape = (cos.shape[0],) + (1,) * (x.ndim - 3) + (half,)
    cs, sn = cos.reshape(shape), sin.reshape(shape)
    xf = x.astype(jnp.float32)
    x1, x2 = xf[..., :half], xf[..., half:]
    return jnp.concatenate([x1 * cs - x2 * sn, x1 * sn + x2 * cs], axis=-1).astype(x.dtype)


def conformer_conv(u, dw_w, dw_b, ln_g, ln_b):
    a, gate = jnp.split(u, 2, axis=-1)
    y = a * jax.nn.sigmoid(gate)
    y = depthwise_conv(y, dw_w, dw_b)
    return jax.nn.silu(layer_norm(y, ln_g, ln_b))


def mla_queries(c_q, q_norm_g, w_uq, cos, sin):
    b, n, _ = c_q.shape
    q = (rms_norm(c_q, q_norm_g) @ w_uq).reshape(b, n, MLA_HEADS, QK_NOPE_DIM + QK_ROPE_DIM)
    q_nope, q_rope = q[..., :QK_NOPE_DIM], q[..., QK_NOPE_DIM:]
    if cos is not None:
        q_rope = apply_rope(q_rope, cos, sin)
    return jnp.concatenate([q_nope, q_rope], axis=-1)


def mla_keys_values(c_kv, k_rope, kv_norm_g, w_ukv, cos, sin):
    b, n, _ = c_kv.shape
    kv = (rms_norm(c_kv, kv_norm_g) @ w_ukv).reshape(b, n, MLA_HEADS, QK_NOPE_DIM + V_HEAD_DIM)
    k_nope, v = kv[..., :QK_NOPE_DIM], kv[..., QK_NOPE_DIM:]
    if cos is not None:
        k_rope = apply_rope(k_rope, cos, sin)
    k_rope = jnp.broadcast_to(k_rope[:, :, None, :], (b, n, MLA_HEADS, QK_ROPE_DIM))
    return jnp.concatenate([k_nope, k_rope], axis=-1), v


def attend(q, k, v):
    b, lq, h, dqk = q.shape
    scale = (QK_NOPE_DIM + QK_ROPE_DIM) ** -0.5

    def block(qb):
        s = jnp.einsum('bqhd,bkhd->bhqk', qb, k, preferred_element_type=jnp.float32) * scale
        p = jax.nn.softmax(s, axis=-1).astype(v.dtype)
        return jnp.einsum('bhqk,bkhd->bqhd', p, v)

    nblk = lq // ATTN_BLOCK
    qb = q.reshape(b, nblk, ATTN_BLOCK, h, dqk).transpose(1, 0, 2, 3, 4)
    out = lax.map(block, qb)
    return out.transpose(1, 0, 2, 3, 4).reshape(b, lq, h * v.shape[-1])


def hyena_filters_freq(n, w1, b1, freq1, w2, b2, freq2, w3):
    pos = jnp.arange(n, dtype=jnp.float32)[:, None]
    t = jnp.linspace(0.0, 1.0, n, dtype=jnp.float32)[:, None]
    w = 2.0 * math.pi * pos / n
    f = jnp.linspace(1e-4, HYENA_BANDS - 1, HYENA_BANDS, dtype=jnp.float32)[None, :]
    emb = jnp.concatenate([t, jnp.cos(f * w), -jnp.sin(f * w)], axis=-1)
    hid = jnp.sin(freq1 * (emb @ w1 + b1))
    hid = jnp.sin(freq2 * (hid @ w2 + b2))
    filt = (hid @ w3).astype(jnp.float32).reshape(n, 2, HYENA_ORDER, HYENA_WIDTH)
    min_decay = math.log(HYENA_DECAY_TARGET) / HYENA_SLOW_DECAY_PCT
    max_decay = math.log(HYENA_DECAY_TARGET) / HYENA_FAST_DECAY_PCT
    deltas = jnp.linspace(min_decay, max_decay, HYENA_WIDTH, dtype=jnp.float32)
    filt = filt * jnp.exp(-t[:, :, None, None] * jnp.abs(deltas))
    full = jnp.concatenate([filt[:, 0], jnp.zeros((1, HYENA_ORDER, HYENA_WIDTH), jnp.float32), filt[:0:-1, 1]], axis=0)
    full = full / jnp.sum(jnp.abs(full), axis=0, keepdims=True)
    return jnp.fft.rfft(full, axis=0)


def hyena_mix(u, short_w, short_b, w1, b1, freq1, w2, b2, freq2, w3, bias):
    n = u.shape[1]
    u = depthwise_conv(u, short_w, short_b)
    x1, x2, v = jnp.split(u.astype(jnp.float32), 3, axis=-1)
    k_f = hyena_filters_freq(n, w1, b1, freq1, w2, b2, freq2, w3)
    z = v
    for o, gate in enumerate((x1, x2)):
        z_f = jnp.fft.rfft(z, n=2 * n, axis=1)
        y = jnp.fft.irfft(z_f * k_f[:, o], n=2 * n, axis=1)[:, :n]
        z = gate * (y + z * bias[o].astype(jnp.float32))
    return z.astype(u.dtype)


def setup_inputs(seed: int = 0) -> dict:
    key = jax.random.key(seed)
    counter = [0]

    def nrm(shape, scale=1.0):
        counter[0] += 1
        return jax.random.normal(jax.random.fold_in(key, counter[0]), shape, jnp.float32) * scale

    def gain(shape):
        return 1.0 + 0.05 * nrm(shape)

    L = DEPTH
    D = D_MODEL
    return {
        'x': nrm((BATCH, SEQ, D)),
        'c': nrm((BATCH, D)),
        'ctx': nrm((BATCH, CTX_LEN, D)),
        'c_ctx': nrm((D,)),
        'w_mod': nrm((L, D, N_MOD * D), 0.5 * D ** -0.5),
        'b_mod': nrm((L, N_MOD * D), 0.02),
        'g_pre_mix': gain((L, D)),
        'g_post_mix': gain((L, D)),
        'g_pre_ffn': gain((L, D)),
        'g_post_ffn': gain((L, D)),
        'w_in': nrm((L, D, IN_WIDTH), D ** -0.5),
        'conv_dw_w': nrm((L, CONV_TAPS, CONV_WIDTH), CONV_TAPS ** -0.5),
        'conv_dw_b': nrm((L, CONV_WIDTH), 0.02),
        'conv_ln_g': gain((L, CONV_WIDTH)),
        'conv_ln_b': nrm((L, CONV_WIDTH), 0.02),
        'mla_q_norm': gain((L, Q_LORA_RANK)),
        'mla_w_uq': nrm((L, Q_LORA_RANK, MLA_HEADS * (QK_NOPE_DIM + QK_ROPE_DIM)), Q_LORA_RANK ** -0.5),
        'mla_kv_norm': gain((L, KV_LORA_RANK)),
        'mla_w_ukv': nrm((L, KV_LORA_RANK, MLA_HEADS * (QK_NOPE_DIM + V_HEAD_DIM)), KV_LORA_RANK ** -0.5),
        'hy_short_w': nrm((L, HYENA_SHORT_TAPS, IN_C), HYENA_SHORT_TAPS ** -0.5),
        'hy_short_b': nrm((L, IN_C), 0.02),
        'hy_w1': nrm((L, HYENA_EMB_DIM, HYENA_FILTER_HIDDEN), HYENA_EMB_DIM ** -0.5),
        'hy_b1': nrm((L, HYENA_FILTER_HIDDEN), 0.02),
        'hy_freq1': gain((L, HYENA_FILTER_HIDDEN)),
        'hy_w2': nrm((L, HYENA_FILTER_HIDDEN, HYENA_FILTER_HIDDEN), HYENA_FILTER_HIDDEN ** -0.5),
        'hy_b2': nrm((L, HYENA_FILTER_HIDDEN), 0.02),
        'hy_freq2': gain((L, HYENA_FILTER_HIDDEN)),
        'hy_w3': nrm((L, HYENA_FILTER_HIDDEN, 2 * HYENA_ORDER * HYENA_WIDTH), HYENA_FILTER_HIDDEN ** -0.5),
        'hy_bias': nrm((L, HYENA_ORDER, HYENA_WIDTH)),
        'w_out': nrm((L, MIX_WIDTH, D), MIX_WIDTH ** -0.5),
        'w_ff1': nrm((L, D, D_FF), D ** -0.5),
        'w_ff2': nrm((L, D_FF, D), D_FF ** -0.5),
    }


def reference(x, c, ctx, c_ctx, w_mod, b_mod, g_pre_mix, g_post_mix, g_pre_ffn, g_post_ffn, w_in,
              conv_dw_w, conv_dw_b, conv_ln_g, conv_ln_b, mla_q_norm, mla_w_uq, mla_kv_norm, mla_w_ukv,
              hy_short_w, hy_short_b, hy_w1, hy_b1, hy_freq1, hy_w2, hy_b2, hy_freq2, hy_w3, hy_bias,
              w_out, w_ff1, w_ff2):
    ROWS = x.shape[1] // GRID_W
    cos, sin = axial_rope_angles(ROWS)
    xc = ctx
    silu_c = jax.nn.silu(c)[:, None, :]
    silu_cc = jax.nn.silu(c_ctx)[None, None, :]
    qa, qb_end = Q_LORA_RANK, Q_LORA_RANK + KV_LORA_RANK

    for l in range(DEPTH):
        last = l == DEPTH - 1
        mx = jnp.split(silu_c @ w_mod[l] + b_mod[l], N_MOD, axis=-1)
        mc = jnp.split(silu_cc @ w_mod[l] + b_mod[l], N_MOD, axis=-1)

        def token_mixer(p_a, attn, p_c):
            conv_o = conformer_conv(p_a, conv_dw_w[l], conv_dw_b[l], conv_ln_g[l], conv_ln_b[l])
            hy_o = hyena_mix(p_c, hy_short_w[l], hy_short_b[l], hy_w1[l], hy_b1[l], hy_freq1[l],
                             hy_w2[l], hy_b2[l], hy_freq2[l], hy_w3[l], hy_bias[l])
            return jnp.concatenate([conv_o, attn, hy_o], axis=-1) @ w_out[l]

        def channel_mixer(stream, m):
            h2 = modulate(rms_norm(stream, g_pre_ffn[l]), m[3], m[4])
            y = jnp.square(jax.nn.relu(h2 @ w_ff1[l])) @ w_ff2[l]
            return stream + m[5] * rms_norm(y, g_post_ffn[l])

        h = modulate(rms_norm(x, g_pre_mix[l]), mx[0], mx[1])
        hc = modulate(rms_norm(xc, g_pre_mix[l]), mc[0], mc[1])

        pc_b = hc @ w_in[l][:, IN_A:IN_A + IN_B]
        kc, vc = mla_keys_values(pc_b[..., qa:qb_end], pc_b[..., qb_end:], mla_kv_norm[l], mla_w_ukv[l], None, None)

        p = h @ w_in[l]
        p_a, p_b, p_c = p[..., :IN_A], p[..., IN_A:IN_A + IN_B], p[..., IN_A + IN_B:]
        q = mla_queries(p_b[..., :qa], mla_q_norm[l], mla_w_uq[l], cos, sin)
        k, v = mla_keys_values(p_b[..., qa:qb_end], p_b[..., qb_end:], mla_kv_norm[l], mla_w_ukv[l], cos, sin)
        attn = attend(q, jnp.concatenate([kc, k], axis=1), jnp.concatenate([vc, v], axis=1))
        new_x = x + mx[2] * rms_norm(token_mixer(p_a, attn, p_c), g_post_mix[l])
        new_x = channel_mixer(new_x, mx)

        if not last:
            pc_a = hc @ w_in[l][:, :IN_A]
            pc_c = hc @ w_in[l][:, IN_A + IN_B:]
            qc = mla_queries(pc_b[..., :qa], mla_q_norm[l], mla_w_uq[l], None, None)
            attn_c = attend(qc, kc, vc)
            xc = xc + mc[2] * rms_norm(token_mixer(pc_a, attn_c, pc_c), g_post_mix[l])
            xc = channel_mixer(xc, mc)
        x = new_x
    return x
```

```python
import numpy as np
import concourse.bass as bass
import concourse.mybir as mybir

F32 = mybir.dt.float32
BF16 = mybir.dt.bfloat16
AF = mybir.ActivationFunctionType
ALU = mybir.AluOpType
AX = mybir.AxisListType


import types


def freeze(fn):
    if fn.__closure__ is None:
        return fn
    cells = []
    for c in fn.__closure__:
        try:
            cells.append(types.CellType(c.cell_contents))
        except ValueError:
            cells.append(c)
    return types.FunctionType(fn.__code__, fn.__globals__, fn.__name__, fn.__defaults__, tuple(cells))


class Tile:
    def __init__(self, prog, t, name):
        self.p = prog
        self.t = t
        self.name = name
        self.w = []
        self.r = []

    def __getitem__(self, k):
        return self.t[k]

    def ap(self):
        return self.t.ap()


class Prog:
    COMPUTE = ("pe", "act", "dve", "pool")

    def __init__(self, nc):
        self.nc = nc
        self.q = {k: [] for k in ("pe", "act", "dve", "pool", "sp")}
        self.esem = {}
        self.ecnt = {}
        self.nsem = 0
        for k in self.COMPUTE:
            self.esem[k] = self.new_sem("e_" + k)
            self.ecnt[k] = 0
        self.waited = {k: {} for k in self.q}
        self._esem_ids = {id(v) for v in self.esem.values()}
        self.sem_latest = {}
        self.sb_off = 16384
        self.KRING = 14
        self.ring = {k: [[self.new_sem(f"r_{k}{i}"), 0] for i in range(self.KRING)] for k in ("sp", "act", "pool")}
        self.ring_i = {k: 0 for k in self.ring}
        self.uid = 0
        self.sb_hi = 0
        self.tiles = []
        self.same_engine_sync = True

    def new_sem(self, name):
        self.nsem += 1
        return self.nc.alloc_semaphore(name=name)

    def sbuf(self, name, shape, dtype):
        nbytes = int(np.prod(shape[1:])) * (2 if dtype == BF16 else 4)
        nbytes = (nbytes + 63) // 64 * 64
        self.uid += 1
        t = self.nc.alloc_sbuf_tensor_at(f"{name}_u{self.uid}", list(shape), dtype, offset=self.sb_off)
        self.sb_off += nbytes
        self.sb_hi = max(self.sb_hi, self.sb_off)
        assert self.sb_off <= 212800, (name, self.sb_off)
        tl = Tile(self, t, name)
        self.tiles.append(tl)
        return tl

    def mark(self):
        return self.sb_off

    def release(self, mark):
        self.sb_off = mark

    def psum(self, name, shape, dtype=F32):
        t = self.nc.alloc_psum_tensor(name, list(shape), dtype)
        tl = Tile(self, t, name)
        self.tiles.append(tl)
        return tl

    def dram(self, name, shape, dtype, kind="Internal"):
        t = self.nc.dram_tensor(name, list(shape), dtype, kind=kind)
        tl = Tile(self, t, name)
        self.tiles.append(tl)
        return tl

    def _deps(self, eng, reads, writes, disjoint=False):
        deps = []
        for t in reads:
            deps.extend(t.w)
        for t in writes:
            if disjoint and not t.r:
                deps.extend(x for x in t.w if id(x[0]) in self._esem_ids)
                deps.extend(getattr(t, "prev_r", []))
            else:
                deps.extend(t.w)
            deps.extend(t.r)
        out = []
        wd = self.waited[eng]
        best = {}
        for (sem, val) in deps:
            sid = id(sem)
            if eng == "pe" and sem is self.esem.get("pe"):
                continue
            if (not self.same_engine_sync) and sem is self.esem.get(eng):
                continue
            if wd.get(sid, 0) >= val:
                continue
            if sid not in best or best[sid][1] < val:
                best[sid] = (sem, val)
        for sid, (sem, val) in best.items():
            wd[sid] = val
            out.append((sem, val))
        return out

    def op(self, eng, fn, reads=(), writes=(), inc=True):
        fn = freeze(fn)
        waits = self._deps(eng, reads, writes)
        sem = self.esem[eng]
        if inc:
            self.ecnt[eng] += 1
        val = self.ecnt[eng] if inc else self.ecnt[eng] + 1
        tok = (sem, val)
        if inc:
            self.sem_latest[id(sem)] = tok
        for t in reads:
            t.r.append(tok)
        for t in writes:
            t.w = [tok]
            t.r = []
            t.prev_r = []

        def thunk(e, waits=waits, fn=fn, sem=sem, inc=inc):
            for (s, v) in waits:
                e.wait_ge(s, v)
            ins = fn(e)
            if inc:
                ins.then_inc(sem, 1)
        self.q[eng].append(thunk)

    def dma(self, eng, dst, dst_ap, src, src_ap, disjoint=False, **kw):
        ring = self.ring[eng]
        slot = ring[self.ring_i[eng] % self.KRING]
        self.ring_i[eng] += 1
        sem, prev = slot
        waits = self._deps(eng, [src], [dst], disjoint=disjoint)
        wd = self.waited[eng]
        if prev > 0 and wd.get(id(sem), 0) < prev:
            wd[id(sem)] = prev
            waits = waits + [(sem, prev)]
        slot[1] = prev + 16
        tok = (sem, prev + 16)
        self.sem_latest[id(sem)] = tok
        src.r.append(tok)
        if dst.r or not disjoint:
            dst.prev_r = list(dst.r) if disjoint else []
            dst.w = [tok]
            dst.r = []
        else:
            dst.w = [x for x in dst.w] + [tok]

        def thunk(e, waits=waits, sem=sem):
            for (s, v) in waits:
                e.wait_ge(s, v)
            try:
                e.dma_start(out=dst_ap, in_=src_ap, **kw).then_inc(sem, 16)
            except Exception:
                print("DMA FAIL", dst.name, src.name, dst_ap, src_ap)
                raise
        self.q[eng].append(thunk)

    def custom(self, eng, fn, reads, writes, sem, inc):
        fn = freeze(fn)
        waits = self._deps(eng, reads, writes)
        cur = getattr(sem, "_cnt", 0) + inc
        try:
            sem._cnt = cur
        except Exception:
            pass
        self._ccnt = getattr(self, "_ccnt", {})
        self._ccnt[id(sem)] = self._ccnt.get(id(sem), 0) + inc
        tok = (sem, self._ccnt[id(sem)])
        self.sem_latest[id(sem)] = tok
        for t in reads:
            t.r.append(tok)
        for t in writes:
            t.w = [tok]
            t.r = []

        def thunk(e, waits=waits):
            for (s, v) in waits:
                e.wait_ge(s, v)
            fn(e).then_inc(sem, inc)
        self.q[eng].append(thunk)

    def barrier(self):
        toks = list(self.sem_latest.values())
        for eng in self.q:
            wd = self.waited[eng]
            ws = []
            for (s, v) in toks:
                if wd.get(id(s), 0) < v:
                    wd[id(s)] = v
                    ws.append((s, v))
            if ws:
                def thunk(e, ws=ws):
                    for (s, v) in ws:
                        e.wait_ge(s, v)
                self.q[eng].append(thunk)
        for t in self.tiles:
            t.w = []
            t.r = []
            t.prev_r = []

    def emit(self):
        nc = self.nc
        with nc.Block() as block:
            @block.tensor
            def _(e):
                for f in self.q["pe"]:
                    f(e)

            @block.scalar
            def _(e):
                for f in self.q["act"]:
                    f(e)

            @block.vector
            def _(e):
                for f in self.q["dve"]:
                    f(e)

            @block.gpsimd
            def _(e):
                for f in self.q["pool"]:
                    f(e)

            @block.sync
            def _(e):
                for f in self.q["sp"]:
                    f(e)
import math
from concourse.bass_utils import run_bass_kernel_spmd

NCORE = 8
D = 2048
SEQ = 16384
TOK = SEQ // NCORE
CTX = 256
KC = D // 128
INW = 3392
EPS = 1e-6
NT = 512
E1_CQ, E1_KV, E1_KR, E1_HY, E1_ROWS = 0, 512, 768, 832, 2368
WIN_GROUPS = [(0, 512, "a"), (512, 512, "g"), (1024, 512, "cq"), (1536, 320, "kv"),
              (1856, 512, "hy0"), (2368, 512, "hy1"), (2880, 512, "hy2")]


def host_prep(inp):
    f32 = np.float32
    x = inp["x"][0]
    com = {}
    com["ctxT"] = np.ascontiguousarray(inp["ctx"][0].T)
    cv = np.stack([inp["c"][0], inp["c_ctx"]], axis=-1)
    com["cvec"] = np.ascontiguousarray(cv.reshape(16, 128, 2).transpose(1, 0, 2).reshape(128, 32))
    g = np.stack([inp["g_pre_mix"], inp["g_post_mix"], inp["g_pre_ffn"], inp["g_post_ffn"]], axis=1)
    com["gcols"] = np.ascontiguousarray(g.reshape(2, 4, 16, 128).transpose(3, 0, 1, 2).reshape(128, 128))
    com["qn"] = np.ascontiguousarray(inp["mla_q_norm"].reshape(2, 4, 128).transpose(2, 0, 1).reshape(128, 8))
    com["kvn"] = np.ascontiguousarray(inp["mla_kv_norm"].reshape(2, 2, 128).transpose(2, 0, 1).reshape(128, 4))
    t = np.arange(SEQ)
    row = (t // 64).astype(np.float64)
    col = (t % 64).astype(np.float64)
    inv = 1.0 / (10000.0 ** (np.arange(0, 32, 2, dtype=np.float64) / 32))
    ang = np.concatenate([row[:, None] * inv, col[:, None] * inv], axis=-1).astype(f32).astype(np.float64)
    cos = np.cos(ang).T
    sin = np.sin(ang).T
    com["rope"] = np.ascontiguousarray(np.stack([np.concatenate([cos, cos], 0), np.concatenate([sin, sin], 0)], 0).astype(f32))
    def hy_tables(n):
        A = n // 128
        dp = np.arange(2 * A)[:, None]; jj = np.arange(128)[None, :]
        off = 128 * (dp - A) + 127 - jj
        pos = np.abs(off).astype(np.float64)
        bad = off == -n
        pos[bad] = 0
        tt = (pos / (n - 1)).astype(f32).astype(np.float64)
        w = (2.0 * np.pi * pos / n)
        fb = np.linspace(1e-4, 15.0, 16)
        emb = np.concatenate([tt[None], np.cos(fb[:, None, None] * w[None]), -np.sin(fb[:, None, None] * w[None])], 0)
        ntx = -tt
        ntx[bad] = -1e4
        return (np.ascontiguousarray(emb.reshape(33, 2 * A * 128).astype(f32)), np.ascontiguousarray(ntx.T.astype(f32)))
    com["embx"], com["ntx"] = hy_tables(SEQ)
    com["embc"], com["ntxc"] = hy_tables(CTX)
    import math
    mind = math.log(1e-2) / 1.5; maxd = math.log(1e-2) / 0.3
    dl_all = np.abs(np.linspace(mind, maxd, 512)).astype(f32)
    hc = np.stack([inp["hy_b1"], inp["hy_freq1"], inp["hy_b2"], inp["hy_freq2"]], -1)
    com["hycol"] = np.ascontiguousarray(hc.transpose(1, 0, 2).reshape(64, 8))
    com["hy_w1"] = inp["hy_w1"]; com["hy_w2"] = inp["hy_w2"]
    cores = []
    for j in range(NCORE):
        d = dict(com)
        for wn in ("w_in", "w_out", "w_ff1", "w_ff2"):
            R = inp[wn].shape[1] // NCORE
            d[wn] = np.ascontiguousarray(inp[wn][:, j * R:(j + 1) * R, :])
        cp = np.concatenate([inp["conv_dw_w"].transpose(0, 2, 1), inp["conv_dw_b"][:, :, None], inp["conv_ln_g"][:, :, None],
                             inp["conv_ln_b"][:, :, None]], axis=-1)
        d["convp"] = np.ascontiguousarray(cp.reshape(2, 4, 128, 34).transpose(2, 0, 1, 3).reshape(128, 2 * 4 * 34))
        chs = slice(64 * j, 64 * j + 64)
        w3 = inp["hy_w3"].reshape(2, 64, 2, 2, 512)[:, :, :, :, chs]
        d["hy_w3c"] = np.ascontiguousarray(w3.reshape(2, 64, 2, 128))
        sw = inp["hy_short_w"].reshape(2, 3, 3, 512)[:, :, :, chs]
        sb = inp["hy_short_b"].reshape(2, 1, 3, 512)[:, :, :, chs]
        swb = np.concatenate([sw, sb], 1).transpose(0, 2, 1, 3)
        d["hy_sw"] = np.ascontiguousarray(swb.reshape(1, 2 * 3 * 4 * 64))
        d["hy_bias_c"] = np.ascontiguousarray(inp["hy_bias"][:, :, chs].reshape(1, 2 * 2 * 64))
        d["hy_dl"] = np.ascontiguousarray(np.tile(dl_all[chs], 2).reshape(1, 128))
        d["w_uq_h"] = np.ascontiguousarray(inp["mla_w_uq"][:, :, j * 192:(j + 1) * 192])
        d["w_ukv_h"] = np.ascontiguousarray(inp["mla_w_ukv"][:, :, j * 256:(j + 1) * 256])
        d["xT"] = np.ascontiguousarray(x[j * TOK:(j + 1) * TOK].T)
        d["ropeL"] = np.ascontiguousarray(com["rope"][:, :, j * TOK:(j + 1) * TOK])
        d["wmod"] = np.ascontiguousarray(inp["w_mod"][:, :, j * 1536:(j + 1) * 1536])
        bm = inp["b_mod"][:, j * 1536:(j + 1) * 1536].reshape(2, 12, 128)
        d["bmod"] = np.ascontiguousarray(bm.transpose(2, 0, 1).reshape(128, 24))
        cores.append(d)
    return cores


def build(stage=99):
    nc = bass.Bass("TRN2", target_bir_lowering=False)
    P = Prog(nc)
    pid = nc.partition_id()

    outs = {}

    def din(name, shape, dt=F32):
        return P.dram(name, shape, dt, kind="ExternalInput")

    xT = din("xT", [D, TOK]); ctxT = din("ctxT", [D, CTX]); cvec = din("cvec", [128, 32])
    wmod = din("wmod", [2, D, 1536]); bmod = din("bmod", [128, 24]); gcols = din("gcols", [128, 128])
    WSPEC = {"w_in": (D, INW), "w_out": (D, D), "w_ff1": (D, 4 * D), "w_ff2": (4 * D, D)}
    need_w = ["w_in"] if stage <= 4 else list(WSPEC)
    wsh = {n: din(n, [2, WSPEC[n][0] // NCORE, WSPEC[n][1]]) for n in need_w}
    qn = din("qn", [128, 8]); kvn = din("kvn", [128, 4])
    rope = din("rope", [2, 64, SEQ]) if stage >= 2 else None
    w_uq_h = din("w_uq_h", [2, 512, 192]) if stage >= 2 else None
    w_ukv_h = din("w_ukv_h", [2, 256, 256]) if stage >= 2 else None
    NKEY = CTX + SEQ
    convp = din("convp", [128, 2 * 4 * 34]) if stage >= 3 else None
    if stage >= 4:
        embx = din("embx", [33, 2 * SEQ]); ntx = din("ntx", [128, 256]); embc = din("embc", [33, 2 * CTX]); ntxc = din("ntxc", [128, 4])
        hycol = din("hycol", [64, 8]); hy_w1 = din("hy_w1", [2, 33, 64]); hy_w2 = din("hy_w2", [2, 64, 64])
        hy_w3c = din("hy_w3c", [2, 64, 2, 128]); hy_sw = din("hy_sw", [1, 1536]); hy_bias_c = din("hy_bias_c", [1, 256])
        hy_dl = din("hy_dl", [1, 128])
        ULp = P.dram("ULp", [3, 64, SEQ + 2], F32); ULc = P.dram("ULc", [3, 64, CTX + 2], F32)
        XG = P.dram("XG", [2, 128, 64, 128], F32); XGc = P.dram("XGc", [2, 2, 64, 128], F32)
        ZD = P.dram("ZD", [64, SEQ + 256], BF16); ZDc = P.dram("ZDc", [64, CTX + 256], BF16)
        HMD = P.dram("HMD", [128, 2, 64, 256], BF16); HMDc = P.dram("HMDc", [128, 2, 64, 4], BF16)
    if stage == 4:
        outs["dbg_hy"] = P.dram("dbg_hy", [64, NKEY], F32, kind="ExternalOutput")
    HBi = P.dram("HBi", [10 * 512, 64], F32); HBo = P.dram("HBo", [10 * 512, 64], F32)
    CO = P.dram("CO", [512, TOK], BF16); COc = P.dram("COc", [512, CTX], BF16)
    if stage == 3:
        outs["dbg_co"] = P.dram("dbg_co", [512, TOK], BF16, kind="ExternalOutput")
        outs["dbg_coc"] = P.dram("dbg_coc", [512, CTX], BF16, kind="ExternalOutput")
    Mloc = P.dram("Mloc", [192, NKEY], F32)
    if stage == 2:
        outs["dbg_attn"] = P.dram("dbg_attn", [128, NKEY], F32, kind="ExternalOutput")

    G0i = P.dram("G0i", [8 * 128, 48], F32); G0o = P.dram("G0o", [8 * 128, 48], F32)
    E1i = P.dram("E1i", [8 * E1_ROWS, TOK], F32); E1o = P.dram("E1o", [8 * E1_ROWS, TOK], F32)
    E1loc = P.dram("E1loc", [E1_ROWS, TOK], F32)
    ropeL = din("ropeL", [2, 64, TOK])
    YG = P.dram("YG", [512, TOK], F32)
    PC = P.dram("PC", [E1_ROWS + 512, CTX], F32)

    if stage <= 1:
        outs["dbg_mod"] = P.dram("dbg_mod", [128, 4 * 96], F32, kind="ExternalOutput")
        outs["dbg_e1"] = P.dram("dbg_e1", [E1_ROWS, TOK], F32, kind="ExternalOutput")
        outs["dbg_yg"] = P.dram("dbg_yg", [512, TOK], F32, kind="ExternalOutput")
        outs["dbg_pc"] = P.dram("dbg_pc", [E1_ROWS + 512, CTX], F32, kind="ExternalOutput")

    banks = [P.psum(f"bank{i}", [128, 512], F32) for i in range(8)]
    bank_i = [0]

    def bank():
        b = banks[bank_i[0] % 8]
        bank_i[0] += 1
        return b

    ones_f = P.sbuf("ones_f", [128, 128], F32)
    P.op("pool", lambda e: e.memset(ones_f[:], 1.0), writes=[ones_f])
    ones_b = P.sbuf("ones_b", [128, 128], BF16)
    P.op("pool", lambda e: e.memset(ones_b[:], 1.0), writes=[ones_b])
    zeros = P.sbuf("zeros", [128, 512], F32)
    P.op("pool", lambda e: e.memset(zeros[:], 0.0), writes=[zeros])
    epsc = P.sbuf("epsc", [128, 1], F32)
    P.op("pool", lambda e: e.memset(epsc[:], EPS), writes=[epsc])
    mc = P.sbuf("mc", [128, 4, 6, 16], F32)
    gcol = P.sbuf("gcol", [128, 2, 4, 16], F32)
    qn_s = P.sbuf("qn_s", [128, 8], F32); kvn_s = P.sbuf("kvn_s", [128, 4], F32)
    P.dma("sp", gcol, gcol[:].rearrange("p a b c -> p (a b c)"), gcols, gcols[:, :])
    P.dma("sp", qn_s, qn_s[:], qn, qn[:, :])
    P.dma("sp", kvn_s, kvn_s[:], kvn, kvn[:, :])

    zeros_bf = P.sbuf("zeros_bf", [128, 1024], BF16)
    P.op("pool", lambda e: e.memset(zeros_bf[:], 0.0), writes=[zeros_bf])

    def zero_fill(dt_tile, nelem, dt):
        per = nelem // 128
        chunk = 512 if dt == F32 else 1024
        assert per % chunk == 0, (dt_tile.name, per)
        z = zeros if dt == F32 else zeros_bf
        rep = per // chunk
        flat = dt_tile.ap().opt() if False else None
        shp = dt_tile.t.shape
        if len(shp) == 2:
            v = dt_tile[:, :].rearrange("r n -> (r n)")
        else:
            v = dt_tile[:, :, :].rearrange("a r n -> (a r n)")
        v = v.rearrange("(p r c) -> p r c", p=128, c=chunk)
        R0 = 0
        while R0 < rep:
            rr = min(64, rep - R0)
            P.dma("sp", dt_tile, v[:, R0:R0 + rr, :], z, z[:].unsqueeze(1).to_broadcast([128, rr, chunk]), disjoint=True)
            R0 += rr

    def allreduce(src, dst, name):
        sem = P.new_sem("cc_" + name)
        P.custom("pool", lambda e: e.collective_compute(
            "AllReduce", ALU.add, replica_groups=[list(range(NCORE))], ins=[src.ap().opt()], outs=[dst.ap().opt()]),
            reads=[src], writes=[dst], sem=sem, inc=1)

    zero_fill(G0i, 8 * 128 * 48 if False else 0, F32) if False else None
    P.dma("sp", G0i, G0i[:, :].rearrange("(j p) f -> p j f", p=128), zeros, zeros[:, 0:384].rearrange("p (j f) -> p j f", j=8))
    zero_fill(E1i, 8 * E1_ROWS * TOK, F32)
    zero_fill(HBi, 10 * 512 * 64, F32)

    WG = {}

    def gather_weight(wn, l):
        R, N = WSPEC[wn]
        loc = P.dram(f"{wn}{l}_loc", [R // NCORE, N], BF16)
        wi = P.dram(f"{wn}{l}_i", [R, N], BF16)
        wo = P.dram(f"{wn}{l}_o", [R, N], BF16)
        zero_fill(wi, R * N, BF16)
        rs_ = R // NCORE
        for r in range(0, rs_, 128):
            P.dma("pool", loc, loc[r:r + 128, :], wsh[wn], wsh[wn][l, r:r + 128, :], disjoint=True)
        P.dma("sp", wi, wi[:, :].rearrange("(j r) n -> j r n", j=NCORE)[bass.ds(pid, 1), :, :].rearrange("a r n -> (a r) n"),
              loc, loc[:, :])
        allreduce(wi, wo, f"{wn}{l}")
        WG[(wn, l)] = wo

    gather_weight("w_in", 0)

    m0 = P.mark()
    cv = P.sbuf("cv", [128, 32], F32); scv = P.sbuf("scv", [128, 32], F32)
    bm_s = P.sbuf("bm_s", [128, 24], F32)
    modloc = P.sbuf("modloc", [128, 2, 2, 12], F32)
    modall = P.sbuf("modall", [128, 4, 96], F32)
    P.dma("sp", cv, cv[:], cvec, cvec[:, :])
    P.dma("sp", bm_s, bm_s[:], bmod, bmod[:, :])
    P.op("act", lambda e: e.activation(out=scv[:], in_=cv[:], func=AF.Silu), reads=[cv], writes=[scv])
    wmt = [P.sbuf(f"wmt{i}", [128, 16, 768], F32) for i in range(2)]
    it = 0
    for l in range(2):
        for half in range(2):
            wt = wmt[it % 2]; it += 1
            src = wmod[l].rearrange("(k p) n -> p k n", p=128)
            for kk in range(0, 16, 4):
                P.dma("sp", wt, wt[:, kk:kk + 4, :], wmod, src[:, kk:kk + 4, half * 768:(half + 1) * 768])
            for c6 in range(6):
                c = half * 6 + c6
                ps = bank()
                for k in range(16):
                    P.op("pe", lambda e, ps=ps, wt=wt, k=k, c6=c6: e.matmul(
                        ps[:, 0:2], wt[:, k, c6 * 128:(c6 + 1) * 128], scv[:, 2 * k:2 * k + 2],
                        start=(k == 0), stop=(k == 15)), reads=[wt, scv], writes=[ps], inc=(k == 15))
                P.op("dve", lambda e, ps=ps, l=l, c=c: e.tensor_scalar(
                    out=modloc[:, l, :, c], in0=ps[:, 0:2], scalar1=bm_s[:, l * 12 + c:l * 12 + c + 1], scalar2=None,
                    op0=ALU.add), reads=[ps, bm_s], writes=[modloc])
    P.dma("pool", G0i, G0i[bass.ds(pid * 128, 128), :], modloc, modloc[:].rearrange("p a b c -> p (a b c)"))
    allreduce(G0i, G0o, "g0")
    for j in range(8):
        P.dma("sp", modall, modall[:, :, j * 12:(j + 1) * 12],
              G0o, G0o[j * 128:(j + 1) * 128, :].rearrange("p (a c) -> p a c", a=4))
    for l in range(2):
        for v in range(2):
            lv = l * 2 + v
            def mv(m, lv=lv):
                return modall[:, lv, m * 16:(m + 1) * 16]
            P.op("dve", lambda e, lv=lv, l=l, mv=mv: e.scalar_tensor_tensor(
                out=mc[:, lv, 0, :], in0=mv(1), scalar=1.0, in1=gcol[:, l, 0, :], op0=ALU.add, op1=ALU.mult),
                reads=[modall, gcol], writes=[mc])
            P.op("dve", lambda e, lv=lv, mv=mv: e.tensor_copy(out=mc[:, lv, 1, :], in_=mv(0)), reads=[modall], writes=[mc])
            P.op("dve", lambda e, lv=lv, l=l, mv=mv: e.tensor_tensor(
                out=mc[:, lv, 2, :], in0=mv(2), in1=gcol[:, l, 1, :], op=ALU.mult), reads=[modall, gcol], writes=[mc])
            P.op("dve", lambda e, lv=lv, l=l, mv=mv: e.scalar_tensor_tensor(
                out=mc[:, lv, 3, :], in0=mv(4), scalar=1.0, in1=gcol[:, l, 2, :], op0=ALU.add, op1=ALU.mult),
                reads=[modall, gcol], writes=[mc])
            P.op("dve", lambda e, lv=lv, mv=mv: e.tensor_copy(out=mc[:, lv, 4, :], in_=mv(3)), reads=[modall], writes=[mc])
            P.op("dve", lambda e, lv=lv, l=l, mv=mv: e.tensor_tensor(
                out=mc[:, lv, 5, :], in0=mv(5), in1=gcol[:, l, 3, :], op=ALU.mult), reads=[modall, gcol], writes=[mc])
    if stage <= 1:
        P.dma("sp", outs["dbg_mod"], outs["dbg_mod"][:, :], mc, mc[:].rearrange("p a b c -> p (a b c)"))
    P.barrier()
    P.release(m0)
    if stage == 0:
        P.emit()
        return nc, list(outs.keys())

    def rstd_from_psum(ps, n, nfeat, rs):
        P.op("act", lambda e: e.activation(out=rs[:, 0:n], in_=ps[:, 0:n], func=AF.Sqrt, scale=1.0 / nfeat, bias=epsc[:, 0:1]),
             reads=[ps, epsc], writes=[rs])
        P.op("dve", lambda e: e.reciprocal(out=rs[:, 0:n], in_=rs[:, 0:n]), reads=[rs], writes=[rs])

    def norm_mod_tile(xt, n, H, h0, lv, kindA, kindS, sq, rs, tmp):
        P.op("act", lambda e: e.activation(out=sq[:, :, 0:n], in_=xt[:, :, 0:n], func=AF.Square), reads=[xt], writes=[sq])
        ps = bank()
        for k in range(KC):
            P.op("pe", lambda e, k=k: e.matmul(ps[:, 0:n], ones_b[:], sq[:, k, 0:n], start=(k == 0), stop=(k == KC - 1)),
                 reads=[sq, ones_b], writes=[ps], inc=(k == KC - 1))
        rstd_from_psum(ps, n, D, rs)
        for k in range(KC):
            tk = tmp[k % 2]
            P.op("dve", lambda e, k=k, tk=tk: e.tensor_tensor(out=tk[:, 0:n], in0=xt[:, k, 0:n], in1=rs[:, 0:n], op=ALU.mult),
                 reads=[xt, rs], writes=[tk])
            P.op("act", lambda e, k=k, tk=tk: e.activation(
                out=H[:, k, h0:h0 + n], in_=tk[:, 0:n], func=AF.Identity,
                scale=mc[:, lv, kindA, k:k + 1], bias=mc[:, lv, kindS, k:k + 1]), reads=[tk, mc], writes=[H])

    def phase_A(l, xsrc, is_first):
        mA = P.mark()
        H = P.sbuf("H", [128, KC, TOK], BF16)
        HC = P.sbuf("HC", [128, KC, CTX], BF16)
        xts = [P.sbuf(f"xt{i}", [128, KC, NT], F32) for i in range(2)]
        sq = P.sbuf("sq", [128, KC, NT], BF16)
        rs = P.sbuf("rs", [128, NT], F32)
        tmp = [P.sbuf(f"tmpA{i}", [128, NT], F32) for i in range(2)]
        for tt in range(TOK // NT):
            xt = xts[tt % 2]
            src = xsrc[:, :].rearrange("(k p) t -> p k t", p=128)
            for kk in range(0, KC, 4):
                P.dma("sp", xt, xt[:, kk:kk + 4, :], xsrc, src[:, kk:kk + 4, tt * NT:(tt + 1) * NT], disjoint=True)
            norm_mod_tile(xt, NT, H, tt * NT, l * 2 + 0, 0, 1, sq, rs, tmp)
        xt = xts[0]
        csrc = (ctxT if is_first else XC)[:, :].rearrange("(k p) t -> p k t", p=128)
        P.dma("sp", xt, xt[:, :, 0:CTX], (ctxT if is_first else XC), csrc)
        norm_mod_tile(xt, CTX, HC, 0, l * 2 + 1, 0, 1, sq, rs, tmp)
        P.barrier()
        P.release(P.mark())
        P.sb_off = mA + (KC * TOK * 2) + (KC * CTX * 2)
        wts = [P.sbuf(f"wtA{i}", [128, KC, 512], BF16) for i in range(2)]
        wrot = P.sbuf("wrot", [128, KC, 64], BF16)
        Abuf = P.sbuf("Abuf", [128, 4, TOK + CTX], F32)
        cqs = P.sbuf("cqs", [128, 4, NT], F32)
        sqc = P.sbuf("sqc", [128, 4, NT], BF16)
        ost = [P.sbuf(f"ost{i}", [128, 4, NT], F32) for i in range(2)]
        rs2 = P.sbuf("rs2", [128, NT], F32)
        sig = P.sbuf("sig", [128, NT], F32)
        cs = P.sbuf("cs", [64, 2, NT], F32)
        t1 = P.sbuf("t1", [64, NT], F32); t2 = P.sbuf("t2", [64, NT], F32)
        oi = [0]
        streams = [(H, t0, NT, False) for t0 in range(0, TOK, NT)] + [(HC, 0, CTX, True)]
        for gi, (c0, ncol, kind) in enumerate(WIN_GROUPS):
            wt = wts[gi % 2]
            wsrc = WG[("w_in", l)][:, :].rearrange("(k p) n -> p k n", p=128)
            for kk in range(0, KC, 4):
                P.dma("sp", wt, wt[:, kk:kk + 4, 0:ncol], WG[("w_in", l)], wsrc[:, kk:kk + 4, c0:c0 + ncol], disjoint=True)
            if kind == "kv":
                P.op("act", lambda e, wt=wt: e.mul(wrot[:, :, 0:32], wt[:, :, 288:320], -1.0),
                     reads=[wt], writes=[wrot])
                P.op("act", lambda e, wt=wt: e.activation(out=wrot[:, :, 32:64], in_=wt[:, :, 256:288], func=AF.Copy),
                     reads=[wt], writes=[wrot])
            for (Hs, t0, n, isctx) in streams:
                nch = (ncol + 127) // 128
                pss = []
                for oc in range(nch):
                    w = min(128, ncol - oc * 128)
                    ps = bank()
                    pss.append((ps, w))
                    for k in range(KC):
                        P.op("pe", lambda e, ps=ps, w=w, oc=oc, k=k, wt=wt, Hs=Hs, t0=t0, n=n: e.matmul(
                            ps[0:w, 0:n], wt[:, k, oc * 128:oc * 128 + w], Hs[:, k, t0:t0 + n],
                            start=(k == 0), stop=(k == KC - 1)), reads=[wt, Hs], writes=[ps], inc=(k == KC - 1))
                o = ost[oi[0] % 2]; oi[0] += 1

                def store(row0, nrows, o_ap, o=o, isctx=isctx, t0=t0, n=n):
                    if isctx:
                        P.dma("sp", PC, PC[row0:row0 + nrows, 0:n], o, o_ap, disjoint=True)
                    else:
                        P.dma("sp", E1loc, E1loc[row0:row0 + nrows, t0:t0 + n], o, o_ap, disjoint=True)

                if kind == "a":
                    for oc, (ps, w) in enumerate(pss):
                        ab0 = (TOK if isctx else t0)
                        P.op("act", lambda e, ps=ps, oc=oc, ab0=ab0, n=n: e.activation(
                            out=Abuf[:, oc, ab0:ab0 + n], in_=ps[:, 0:n], func=AF.Copy), reads=[ps], writes=[Abuf])
                    oi[0] -= 1
                elif kind == "g":
                    ab0 = (TOK if isctx else t0)
                    for oc, (ps, w) in enumerate(pss):
                        P.op("act", lambda e, ps=ps, n=n: e.activation(out=sig[:, 0:n], in_=ps[:, 0:n], func=AF.Sigmoid),
                             reads=[ps], writes=[sig])
                        P.op("dve", lambda e, oc=oc, o=o, ab0=ab0, n=n: e.tensor_tensor(
                            out=o[:, oc, 0:n], in0=Abuf[:, oc, ab0:ab0 + n], in1=sig[:, 0:n], op=ALU.mult),
                            reads=[Abuf, sig], writes=[o])
                    if isctx:
                        P.dma("sp", PC, PC[E1_ROWS:E1_ROWS + 512, :].rearrange("(c p) t -> p c t", p=128), o, o[:, :, 0:n])
                    else:
                        P.dma("sp", YG, YG[:, t0:t0 + n].rearrange("(c p) t -> p c t", p=128), o, o[:, :, 0:n])
                elif kind in ("cq", "kv"):
                    nn = 4 if kind == "cq" else 2
                    for oc in range(nn):
                        ps = pss[oc][0]
                        P.op("act", lambda e, ps=ps, oc=oc, n=n: e.activation(out=cqs[:, oc, 0:n], in_=ps[:, 0:n], func=AF.Copy),
                             reads=[ps], writes=[cqs])
                        P.op("act", lambda e, ps=ps, oc=oc, n=n: e.activation(out=sqc[:, oc, 0:n], in_=ps[:, 0:n], func=AF.Square),
                             reads=[ps], writes=[sqc])
                    ps2 = bank()
                    for oc in range(nn):
                        P.op("pe", lambda e, ps2=ps2, oc=oc, n=n, nn=nn: e.matmul(
                            ps2[:, 0:n], ones_b[:], sqc[:, oc, 0:n], start=(oc == 0), stop=(oc == nn - 1)),
                            reads=[sqc, ones_b], writes=[ps2], inc=(oc == nn - 1))
                    rstd_from_psum(ps2, n, 128 * nn, rs2)
                    gt = qn_s if kind == "cq" else kvn_s
                    for oc in range(nn):
                        P.op("dve", lambda e, oc=oc, o=o, n=n, gt=gt, nn=nn: e.scalar_tensor_tensor(
                            out=o[:, oc, 0:n], in0=cqs[:, oc, 0:n], scalar=gt[:, l * nn + oc:l * nn + oc + 1],
                            in1=rs2[:, 0:n], op0=ALU.mult, op1=ALU.mult), reads=[cqs, gt, rs2], writes=[o])
                    row0 = E1_CQ if kind == "cq" else E1_KV
                    if isctx:
                        P.dma("sp", PC, PC[row0:row0 + 128 * nn, :].rearrange("(c p) t -> p c t", p=128), o, o[:, 0:nn, 0:n])
                    else:
                        for oc in range(nn):
                            store(row0 + oc * 128, 128, o[:, oc, 0:n])
                    if kind == "kv":
                        psr = pss[2][0]
                        if isctx:
                            P.op("act", lambda e, psr=psr, o=o, n=n: e.activation(out=o[0:64, 2, 0:n], in_=psr[0:64, 0:n], func=AF.Copy),
                                 reads=[psr], writes=[o])
                        else:
                            for a in range(2):
                                P.dma("sp", cs, cs[:, a, 0:n], ropeL, ropeL[a, :, t0:t0 + n])
                            ps3 = bank()
                            for k in range(KC):
                                P.op("pe", lambda e, ps3=ps3, k=k, Hs=Hs, t0=t0, n=n: e.matmul(
                                    ps3[0:64, 0:n], wrot[:, k, :], Hs[:, k, t0:t0 + n], start=(k == 0), stop=(k == KC - 1)),
                                    reads=[wrot, Hs], writes=[ps3], inc=(k == KC - 1))
                            P.op("dve", lambda e, psr=psr, t0=t0, n=n: e.tensor_tensor(
                                out=t1[:, 0:n], in0=psr[0:64, 0:n], in1=cs[:, 0, 0:n], op=ALU.mult), reads=[psr, cs], writes=[t1])
                            P.op("dve", lambda e, ps3=ps3, t0=t0, n=n: e.tensor_tensor(
                                out=t2[:, 0:n], in0=ps3[0:64, 0:n], in1=cs[:, 1, 0:n], op=ALU.mult), reads=[ps3, cs], writes=[t2])
                            P.op("dve", lambda e, o=o, n=n: e.tensor_tensor(
                                out=o[0:64, 2, 0:n], in0=t1[:, 0:n], in1=t2[:, 0:n], op=ALU.add), reads=[t1, t2], writes=[o])
                        store(E1_KR, 64, o[0:64, 2, 0:n])
                else:
                    hi = int(kind[2])
                    for oc, (ps, w) in enumerate(pss):
                        eng = "act" if oc % 2 == 0 else "dve"
                        if eng == "act":
                            P.op("act", lambda e, ps=ps, oc=oc, o=o, n=n: e.activation(out=o[:, oc, 0:n], in_=ps[:, 0:n], func=AF.Copy),
                                 reads=[ps], writes=[o])
                        else:
                            P.op("dve", lambda e, ps=ps, oc=oc, o=o, n=n: e.tensor_copy(out=o[:, oc, 0:n], in_=ps[:, 0:n]),
                                 reads=[ps], writes=[o])
                    row0 = E1_HY + hi * 512
                    if isctx:
                        P.dma("sp", PC, PC[row0:row0 + 512, :].rearrange("(c p) t -> p c t", p=128), o, o[:, :, 0:n])
                    else:
                        for oc in range(4):
                            store(row0 + oc * 128, 128, o[:, oc, 0:n])
        P.dma("pool", E1i, E1i[:, :].rearrange("(j q r) t -> j q (r t)", j=8, q=64)[bass.ds(pid, 1), :, :].rearrange("a q f -> (a q) f"),
              E1loc, E1loc[:, :].rearrange("(q r) t -> q (r t)", q=64))
        P.barrier()
        P.release(mA)


    SCALE = 192.0 ** -0.5
    NCH = NKEY // 128

    def e1_rows(row0, nrows, slot, c0, n):
        return E1o[slot * E1_ROWS + row0:slot * E1_ROWS + row0 + nrows, c0:c0 + n]

    def phase_B1(l, do_ctx_q):
        mB = P.mark()
        KN = P.sbuf("KN", [128, NKEY], BF16)
        KR = P.sbuf("KR", [64, NKEY], BF16)
        V = P.sbuf("V", [128, NCH, 128], BF16)
        wq = P.sbuf("wq", [128, 4, 192], BF16); wqr = P.sbuf("wqr", [128, 4, 64], BF16)
        wkv = P.sbuf("wkv", [128, 2, 256], BF16)
        ckvT = [P.sbuf(f"ckvT{i}", [128, 2, NT], BF16) for i in range(2)]
        cqT = [P.sbuf(f"cqT{i}", [128, 4, NT], BF16) for i in range(2)]
        QNs = [P.sbuf(f"QNs{i}", [128, NT], BF16) for i in range(2)]
        QRs = [P.sbuf(f"QRs{i}", [64, NT], BF16) for i in range(2)]
        csq = [P.sbuf(f"csq{i}", [64, 2, NT], F32) for i in range(2)]
        q1 = P.sbuf("q1", [64, NT], F32); q2 = P.sbuf("q2", [64, NT], F32)
        PT = [P.sbuf(f"PT{i}", [128, NT], BF16) for i in range(3)]
        acc = [P.sbuf(f"acc{i}", [128, NT], F32) for i in range(2)]
        rinv = P.sbuf("rinv", [128, NT], F32)
        Oo = [P.sbuf(f"Oo{i}", [128, NT], F32) for i in range(2)]
        P.dma("pool", wq, wq[:], w_uq_h, w_uq_h[l].rearrange("(k p) n -> p k n", p=128))
        P.dma("pool", wkv, wkv[:], w_ukv_h, w_ukv_h[l].rearrange("(k p) n -> p k n", p=128))
        P.op("act", lambda e: e.mul(wqr[:, :, 0:32], wq[:, :, 160:192], -1.0), reads=[wq], writes=[wqr])
        P.op("act", lambda e: e.activation(out=wqr[:, :, 32:64], in_=wq[:, :, 128:160], func=AF.Copy), reads=[wq], writes=[wqr])
        ktiles = [("ctx", 0, CTX, 0)] + [(t // 4, (t % 4) * NT, NT, CTX + t * NT) for t in range(SEQ // NT)]
        for i, (slot, c0, n, k0) in enumerate(ktiles):
            ck = ckvT[i % 2]
            if slot == "ctx":
                P.dma("pool", ck, ck[:, :, 0:n], PC, PC[E1_KV:E1_KV + 256, 0:n].rearrange("(c p) t -> p c t", p=128))
                P.dma("pool", KR, KR[:, k0:k0 + n], PC, PC[E1_KR:E1_KR + 64, 0:n], disjoint=True)
            else:
                P.dma("pool", ck, ck[:, :, 0:n], E1o, e1_rows(E1_KV, 256, slot, c0, n).rearrange("(c p) t -> p c t", p=128))
                P.dma("pool", KR, KR[:, k0:k0 + n], E1o, e1_rows(E1_KR, 64, slot, c0, n), disjoint=True)
            ps = bank()
            for c in range(2):
                P.op("pe", lambda e, ps=ps, c=c, ck=ck, n=n: e.matmul(ps[:, 0:n], wkv[:, c, 0:128], ck[:, c, 0:n],
                                                                     start=(c == 0), stop=(c == 1)),
                     reads=[wkv, ck], writes=[ps], inc=(c == 1))
            P.op("act", lambda e, ps=ps, k0=k0, n=n: e.activation(out=KN[:, k0:k0 + n], in_=ps[:, 0:n], func=AF.Copy),
                 reads=[ps], writes=[KN])
            ps2 = bank()
            nsub = n // 128
            for sc in range(nsub):
                for c in range(2):
                    P.op("pe", lambda e, ps2=ps2, sc=sc, c=c, ck=ck: e.matmul(
                        ps2[:, sc * 128:(sc + 1) * 128], ck[:, c, sc * 128:(sc + 1) * 128], wkv[:, c, 128:256],
                        start=(c == 0), stop=(c == 1)), reads=[wkv, ck], writes=[ps2], inc=(c == 1 and sc == nsub - 1))
            ch0 = k0 // 128
            P.op("dve", lambda e, ps2=ps2, ch0=ch0, nsub=nsub: e.tensor_copy(
                out=V[:, ch0:ch0 + nsub, :], in_=ps2[:, 0:nsub * 128].rearrange("p (a b) -> p a b", b=128)),
                reads=[ps2], writes=[V])
        qtiles = [(t // 4, (t % 4) * NT, NT, t * NT, NCH) for t in range(SEQ // NT)]
        if do_ctx_q:
            qtiles.append(("ctx", 0, CTX, SEQ, 2))
        for i, (slot, c0, n, m0, nch) in enumerate(qtiles):
            cq = cqT[i % 2]; qn_ = QNs[i % 2]; qr_ = QRs[i % 2]; cs_ = csq[i % 2]; ac = acc[i % 2]; oo = Oo[i % 2]
            if slot == "ctx":
                P.dma("pool", cq, cq[:, :, 0:n], PC, PC[E1_CQ:E1_CQ + 512, 0:n].rearrange("(c p) t -> p c t", p=128))
            else:
                P.dma("pool", cq, cq[:, :, 0:n], E1o, e1_rows(E1_CQ, 512, slot, c0, n).rearrange("(c p) t -> p c t", p=128))
                for a in range(2):
                    P.dma("sp", cs_, cs_[:, a, 0:n], rope, rope[a, :, m0:m0 + n])
            psn = bank()
            for k in range(4):
                P.op("pe", lambda e, psn=psn, k=k, cq=cq, n=n: e.matmul(psn[:, 0:n], wq[:, k, 0:128], cq[:, k, 0:n],
                                                                       start=(k == 0), stop=(k == 3)),
                     reads=[wq, cq], writes=[psn], inc=(k == 3))
            P.op("act", lambda e, psn=psn, qn_=qn_, n=n: e.activation(out=qn_[:, 0:n], in_=psn[:, 0:n], func=AF.Copy),
                 reads=[psn], writes=[qn_])
            psp = bank()
            for k in range(4):
                P.op("pe", lambda e, psp=psp, k=k, cq=cq, n=n: e.matmul(psp[0:64, 0:n], wq[:, k, 128:192], cq[:, k, 0:n],
                                                                       start=(k == 0), stop=(k == 3)),
                     reads=[wq, cq], writes=[psp], inc=(k == 3))
            if slot == "ctx":
                P.op("act", lambda e, psp=psp, qr_=qr_, n=n: e.activation(out=qr_[:, 0:n], in_=psp[0:64, 0:n], func=AF.Copy),
                     reads=[psp], writes=[qr_])
            else:
                psr = bank()
                for k in range(4):
                    P.op("pe", lambda e, psr=psr, k=k, cq=cq, n=n: e.matmul(psr[0:64, 0:n], wqr[:, k, :], cq[:, k, 0:n],
                                                                           start=(k == 0), stop=(k == 3)),
                         reads=[wqr, cq], writes=[psr], inc=(k == 3))
                P.op("dve", lambda e, psp=psp, cs_=cs_, n=n: e.tensor_tensor(out=q1[:, 0:n], in0=psp[0:64, 0:n], in1=cs_[:, 0, 0:n], op=ALU.mult),
                     reads=[psp, cs_], writes=[q1])
                P.op("dve", lambda e, psr=psr, cs_=cs_, n=n: e.tensor_tensor(out=q2[:, 0:n], in0=psr[0:64, 0:n], in1=cs_[:, 1, 0:n], op=ALU.mult),
                     reads=[psr, cs_], writes=[q2])
                P.op("dve", lambda e, qr_=qr_, n=n: e.tensor_tensor(out=qr_[:, 0:n], in0=q1[:, 0:n], in1=q2[:, 0:n], op=ALU.add),
                     reads=[q1, q2], writes=[qr_])
            pso = bank()
            for c in range(nch):
                pst = bank()
                if pst is pso:
                    pst = bank()
                pt = PT[c % 3]
                P.op("pe", lambda e, pst=pst, c=c, qn_=qn_, n=n: e.matmul(pst[:, 0:n], KN[:, c * 128:(c + 1) * 128], qn_[:, 0:n],
                                                                         start=True, stop=False),
                     reads=[KN, qn_], writes=[pst], inc=False)
                P.op("pe", lambda e, pst=pst, c=c, qr_=qr_, n=n: e.matmul(pst[:, 0:n], KR[:, c * 128:(c + 1) * 128], qr_[:, 0:n],
                                                                         start=False, stop=True),
                     reads=[KR, qr_], writes=[pst])
                P.op("act", lambda e, pst=pst, pt=pt, n=n: e.activation(out=pt[:, 0:n], in_=pst[:, 0:n], func=AF.Exp, scale=SCALE),
                     reads=[pst], writes=[pt])
                if c == 0:
                    P.op("dve", lambda e, pt=pt, ac=ac, n=n: e.tensor_copy(out=ac[:, 0:n], in_=pt[:, 0:n]), reads=[pt], writes=[ac])
                else:
                    P.op("dve", lambda e, pt=pt, ac=ac, n=n: e.tensor_tensor(out=ac[:, 0:n], in0=ac[:, 0:n], in1=pt[:, 0:n], op=ALU.add),
                         reads=[pt, ac], writes=[ac])
                P.op("pe", lambda e, pso=pso, c=c, pt=pt, n=n, nch=nch: e.matmul(pso[:, 0:n], V[:, c, :], pt[:, 0:n],
                                                                                start=(c == 0), stop=(c == nch - 1)),
                     reads=[V, pt], writes=[pso], inc=(c == nch - 1))
            pss = bank()
            if pss is pso:
                pss = bank()
            P.op("pe", lambda e, pss=pss, ac=ac, n=n: e.matmul(pss[:, 0:n], ones_f[:], ac[:, 0:n], start=True, stop=True),
                 reads=[ones_f, ac], writes=[pss])
            P.op("dve", lambda e, pss=pss, n=n: e.reciprocal(out=rinv[:, 0:n], in_=pss[:, 0:n]), reads=[pss], writes=[rinv])
            P.op("dve", lambda e, pso=pso, oo=oo, n=n: e.tensor_tensor(out=oo[:, 0:n], in0=pso[:, 0:n], in1=rinv[:, 0:n], op=ALU.mult),
                 reads=[pso, rinv], writes=[oo])
            P.dma("sp", Mloc, Mloc[0:128, m0:m0 + n], oo, oo[:, 0:n], disjoint=True)
        P.barrier()
        P.release(mB)

    def halo_exchange(l=0):
        hb = HBi[:, :].rearrange("(j r) c -> j r c", j=10)[bass.ds(pid + 1, 1), :, :].rearrange("a r c -> (a r) c")
        P.dma("sp", HBi, hb[:, 0:32], YG, YG[:, 0:32])
        P.dma("sp", HBi, hb[:, 32:64], YG, YG[:, TOK - 32:TOK])
        allreduce(HBi, HBo, f"hb{l}")

    def phase_B2(l, do_ctx=True):
        mB = P.mark()
        cvp = P.sbuf("cvp", [128, 4, 34], F32)
        P.dma("sp", cvp, cvp[:].rearrange("p a b -> p (a b)"), convp, convp[:, l * 136:(l + 1) * 136])
        Yp = P.sbuf("Yp", [128, 4, TOK + 32], F32)
        ac = P.sbuf("cacc", [128, 4, TOK], F32)
        sqv = [P.sbuf(f"sqv{i}", [128, NT], F32) for i in range(2)]
        mean = P.sbuf("cmean", [128, NT], F32); rstd = P.sbuf("crstd", [128, NT], F32); msq = P.sbuf("cmsq", [128, NT], F32)
        tt_ = [P.sbuf(f"ctt{i}", [128, NT], F32) for i in range(2)]
        cob = [P.sbuf(f"cob{i}", [128, 4, NT], BF16) for i in range(2)]
        for (isctx, ntok) in (((False, TOK), (True, CTX)) if do_ctx else ((False, TOK),)):
            if isctx:
                P.op("pool", lambda e: e.memset(Yp[:, :, 0:16], 0.0), writes=[Yp])
                P.op("pool", lambda e: e.memset(Yp[:, :, 16 + CTX:32 + CTX], 0.0), writes=[Yp])
                P.dma("sp", Yp, Yp[:, :, 16:16 + CTX], PC, PC[E1_ROWS:E1_ROWS + 512, :].rearrange("(c p) t -> p c t", p=128))
            else:
                hl = HBo[:, :].rearrange("(j r) c -> j r c", j=10)[bass.ds(pid, 1), :, :].rearrange("a r c -> (a r) c")
                hr = HBo[:, :].rearrange("(j r) c -> j r c", j=10)[bass.ds(pid + 2, 1), :, :].rearrange("a r c -> (a r) c")
                P.dma("sp", Yp, Yp[:, :, 0:16], HBo, hl[:, 48:64].rearrange("(c p) t -> p c t", p=128))
                P.dma("sp", Yp, Yp[:, :, 16 + TOK:32 + TOK], HBo, hr[:, 0:16].rearrange("(c p) t -> p c t", p=128))
                P.dma("sp", Yp, Yp[:, :, 16:16 + TOK], YG, YG[:, :].rearrange("(c p) t -> p c t", p=128))
            for c in range(4):
                eng = "dve"
                P.op(eng, lambda e, c=c: e.tensor_scalar(out=ac[:, c, 0:ntok], in0=Yp[:, c, 1:1 + ntok], scalar1=cvp[:, c, 0:1],
                                                        scalar2=cvp[:, c, 31:32], op0=ALU.mult, op1=ALU.add),
                     reads=[Yp, cvp], writes=[ac])
                for k in range(1, 31):
                    P.op(eng, lambda e, c=c, k=k: e.scalar_tensor_tensor(
                        out=ac[:, c, 0:ntok], in0=Yp[:, c, k + 1:k + 1 + ntok], scalar=cvp[:, c, k:k + 1], in1=ac[:, c, 0:ntok],
                        op0=ALU.mult, op1=ALU.add), reads=[Yp, cvp, ac], writes=[ac])
            for t0 in range(0, ntok, NT):
                n = min(NT, ntok - t0)
                ps1 = bank(); ps2 = bank()
                for c in range(4):
                    sq_ = sqv[c % 2]
                    P.op("act", lambda e, c=c, sq_=sq_: e.activation(out=sq_[:, 0:n], in_=ac[:, c, t0:t0 + n], func=AF.Square),
                         reads=[ac], writes=[sq_])
                    P.op("pe", lambda e, c=c: e.matmul(ps1[:, 0:n], ones_f[:], ac[:, c, t0:t0 + n], start=(c == 0), stop=(c == 3)),
                         reads=[ones_f, ac], writes=[ps1], inc=(c == 3))
                    P.op("pe", lambda e, c=c, sq_=sq_: e.matmul(ps2[:, 0:n], ones_f[:], sq_[:, 0:n], start=(c == 0), stop=(c == 3)),
                         reads=[ones_f, sq_], writes=[ps2])
                P.op("act", lambda e: e.mul(mean[:, 0:n], ps1[:, 0:n], 1.0 / 512), reads=[ps1], writes=[mean])
                P.op("dve", lambda e: e.tensor_tensor(out=msq[:, 0:n], in0=mean[:, 0:n], in1=mean[:, 0:n], op=ALU.mult),
                     reads=[mean], writes=[msq])
                P.op("dve", lambda e: e.scalar_tensor_tensor(out=rstd[:, 0:n], in0=ps2[:, 0:n], scalar=1.0 / 512, in1=msq[:, 0:n],
                                                            op0=ALU.mult, op1=ALU.subtract), reads=[ps2, msq], writes=[rstd])
                P.op("act", lambda e: e.activation(out=rstd[:, 0:n], in_=rstd[:, 0:n], func=AF.Sqrt, bias=epsc[:, 0:1]),
                     reads=[rstd, epsc], writes=[rstd])
                P.op("dve", lambda e: e.reciprocal(out=rstd[:, 0:n], in_=rstd[:, 0:n]), reads=[rstd], writes=[rstd])
                ob = cob[(t0 // NT) % 2]
                for c in range(4):
                    t_ = tt_[c % 2]
                    P.op("dve", lambda e, c=c, t_=t_: e.tensor_tensor(out=t_[:, 0:n], in0=ac[:, c, t0:t0 + n], in1=mean[:, 0:n], op=ALU.subtract),
                         reads=[ac, mean], writes=[t_])
                    P.op("dve", lambda e, t_=t_: e.tensor_tensor(out=t_[:, 0:n], in0=t_[:, 0:n], in1=rstd[:, 0:n], op=ALU.mult),
                         reads=[t_, rstd], writes=[t_])
                    P.op("act", lambda e, c=c, t_=t_, ob=ob: e.activation(out=ob[:, c, 0:n], in_=t_[:, 0:n], func=AF.Silu,
                                                                       scale=cvp[:, c, 32:33], bias=cvp[:, c, 33:34]),
                         reads=[t_, cvp], writes=[ob])
                dst = COc if isctx else CO
                P.dma("sp", dst, dst[:, t0:t0 + n].rearrange("(c p) t -> p c t", p=128), ob, ob[:, :, 0:n])
        P.barrier()
        P.release(mB)

    MAGIC = 12582912.0
    TWO_PI = 2.0 * math.pi

    def hy_init():
        for g in range(3):
            for (t, n_) in ((ULp, SEQ), (ULc, CTX)):
                P.dma("sp", t, t[g, :, 0:1], zeros, zeros[0:64, 0:1], allow_slow_non_contiguous=True)
                P.dma("sp", t, t[g, :, n_ + 1:n_ + 2], zeros, zeros[0:64, 0:1], allow_slow_non_contiguous=True)
        for (t, n_) in ((ZD, SEQ), (ZDc, CTX)):
            P.dma("sp", t, t[:, 0:128], zeros_bf, zeros_bf[0:64, 0:128])
            P.dma("sp", t, t[:, 128 + n_:256 + n_], zeros_bf, zeros_bf[0:64, 0:128])

    def sin_layer(ps, bcol, fcol, out, tmpa, tmpb, n):
        P.op("dve", lambda e: e.tensor_scalar(out=tmpa[:, 0:n], in0=ps[0:64, 0:n], scalar1=bcol, scalar2=fcol, op0=ALU.add, op1=ALU.mult),
             reads=[ps, hyc], writes=[tmpa])
        P.op("dve", lambda e: e.tensor_scalar(out=tmpb[:, 0:n], in0=tmpa[:, 0:n], scalar1=1.0 / TWO_PI, scalar2=MAGIC, op0=ALU.mult, op1=ALU.add),
             reads=[tmpa], writes=[tmpb])
        P.op("dve", lambda e: e.tensor_scalar(out=tmpb[:, 0:n], in0=tmpb[:, 0:n], scalar1=MAGIC, scalar2=None, op0=ALU.subtract),
             reads=[tmpb], writes=[tmpb])
        P.op("dve", lambda e: e.scalar_tensor_tensor(out=tmpa[:, 0:n], in0=tmpb[:, 0:n], scalar=-TWO_PI, in1=tmpa[:, 0:n], op0=ALU.mult, op1=ALU.add),
             reads=[tmpa, tmpb], writes=[tmpa])
        P.op("dve", lambda e: e.tensor_scalar(out=tmpa[:, 0:n], in0=tmpa[:, 0:n], scalar1=-math.pi, scalar2=math.pi, op0=ALU.max, op1=ALU.min),
             reads=[tmpa], writes=[tmpa])
        P.op("act", lambda e: e.activation(out=out[:, 0:n], in_=tmpa[:, 0:n], func=AF.Sin), reads=[tmpa], writes=[out])

    hyc = None

    def hy_filters(l, A, emb_t, ntx_t, hmd, rn):
        nonlocal hyc
        mF = P.mark()
        hyc = P.sbuf("hyc", [64, 8], F32)
        w1s = P.sbuf("w1s", [33, 64], F32); w2s = P.sbuf("w2s", [64, 64], F32); w3s = P.sbuf("w3s", [64, 2, 128], F32)
        dlb = P.sbuf("dlb", [128, 128], F32); ntxs = P.sbuf("ntxs", [128, 2 * A], F32)
        P.dma("sp", hyc, hyc[:], hycol, hycol[:, :])
        P.dma("sp", w1s, w1s[:], hy_w1, hy_w1[l]); P.dma("sp", w2s, w2s[:], hy_w2, hy_w2[l])
        P.dma("sp", w3s, w3s[:], hy_w3c, hy_w3c[l])
        P.dma("sp", dlb, dlb[:], hy_dl, hy_dl[0:1, :].partition_broadcast(128))
        P.dma("sp", ntxs, ntxs[:], ntx_t, ntx_t[:, :])
        embs = [P.sbuf(f"embs{i}", [33, NT], F32) for i in range(2)]
        ta = P.sbuf("hta", [64, NT], F32); tb = P.sbuf("htb", [64, NT], F32)
        s1 = P.sbuf("hs1", [64, NT], F32); s2 = P.sbuf("hs2", [64, NT], F32)
        dec = [P.sbuf(f"hdec{i}", [128, 128], F32) for i in range(2)]
        hv = [P.sbuf(f"hhv{i}", [128, 128], F32) for i in range(2)]
        nacc = P.sbuf("nacc", [128, 128], F32)
        hst = P.sbuf("hst", [128, 128, 2 * A], BF16)
        P.op("dve", lambda e: e.memset(nacc[:], 0.0), writes=[nacc])
        ngrp = (2 * A * 128) // NT
        for gi in range(ngrp):
            em = embs[gi % 2]
            P.dma("sp", em, em[:], emb_t, emb_t[:, gi * NT:(gi + 1) * NT])
            ps = bank()
            P.op("pe", lambda e: e.matmul(ps[0:64, 0:NT], w1s[:], em[:], start=True, stop=True), reads=[w1s, em], writes=[ps])
            sin_layer(ps, hyc[:, l * 4 + 0:l * 4 + 1], hyc[:, l * 4 + 1:l * 4 + 2], s1, ta, tb, NT)
            ps = bank()
            P.op("pe", lambda e: e.matmul(ps[0:64, 0:NT], w2s[:], s1[:], start=True, stop=True), reads=[w2s, s1], writes=[ps])
            sin_layer(ps, hyc[:, l * 4 + 2:l * 4 + 3], hyc[:, l * 4 + 3:l * 4 + 4], s2, ta, tb, NT)
            for sub in range(4):
                dpi = gi * 4 + sub
                dr = 1 if dpi < A else 0
                psf = bank()
                P.op("pe", lambda e: e.matmul(psf[:, 0:128], s2[:, sub * 128:(sub + 1) * 128], w3s[:, dr, :], start=True, stop=True),
                     reads=[s2, w3s], writes=[psf])
                dc = dec[dpi % 2]; h_ = hv[dpi % 2]
                P.op("act", lambda e: e.activation(out=dc[:], in_=dlb[:], func=AF.Exp, scale=ntxs[:, dpi:dpi + 1]), reads=[dlb, ntxs], writes=[dc])
                P.op("dve", lambda e: e.tensor_tensor(out=h_[:], in0=psf[:, 0:128], in1=dc[:], op=ALU.mult), reads=[psf, dc], writes=[h_])
                P.op("act", lambda e: e.activation(out=dc[:], in_=h_[:], func=AF.Abs), reads=[h_], writes=[dc])
                P.op("dve", lambda e: e.tensor_tensor(out=nacc[:], in0=nacc[:], in1=dc[:], op=ALU.add), reads=[dc, nacc], writes=[nacc])
                P.op("act", lambda e: e.activation(out=hst[:, :, dpi], in_=h_[:], func=AF.Copy), reads=[h_], writes=[hst])
        P.dma("sp", hmd, hmd[:, :, :, :].rearrange("p o c d -> p (o c) d"), hst, hst[:])
        psn = bank()
        P.op("pe", lambda e: e.matmul(psn[:, 0:128], ones_f[:], nacc[:], start=True, stop=True), reads=[ones_f, nacc], writes=[psn])
        P.op("dve", lambda e: e.reciprocal(out=rn[:], in_=psn[:, 0:128]), reads=[psn], writes=[rn])
        P.barrier()
        P.release(mF)

    def hyena_stream(l, A, n, UL, xg, zd, hmd, rn, mcol0, src_is_ctx):
        mH = P.mark()
        AP_ = bass.AP
        swp = P.sbuf("swp", [128, 3, 4, 64], F32); hbs = P.sbuf("hbs", [128, 2, 64], F32)
        P.dma("sp", swp, swp[:].rearrange("p a b c -> p (a b c)"), hy_sw, hy_sw[0:1, l * 768:(l + 1) * 768].partition_broadcast(128))
        P.dma("sp", hbs, hbs[:].rearrange("p a c -> p (a c)"), hy_bias_c, hy_bias_c[0:1, l * 128:(l + 1) * 128].partition_broadcast(128))
        for g in range(3):
            if src_is_ctx:
                src = PC[E1_HY + g * 512:E1_HY + (g + 1) * 512, :].rearrange("(j c) t -> j c t", j=8)[bass.ds(pid, 1), :, :].rearrange("a c t -> (a c) t")
                P.dma("sp", UL, UL[g, :, 1:1 + n], PC, src)
            else:
                src = E1o[:, :].rearrange("(r f) t -> r f t", r=8)[:, E1_HY + g * 512:E1_HY + (g + 1) * 512, :]
                src = src.rearrange("r (j c) t -> j c r t", j=8)[bass.ds(pid, 1), :, :, :].rearrange("a c r t -> (a c) r t")
                P.dma("sp", UL, UL[g, :, 1:1 + n].rearrange("c (r t) -> c r t", r=8), E1o, src)
        Z = P.sbuf("Zh", [A, 64, 128], F32)
        Y = P.sbuf("Yh", [A, 64, 128], F32)
        mU = P.mark()
        Ur = P.sbuf("Ur", [A, 64, 130], F32)
        ROWU = n + 2
        for g in range(3):
            src = AP_(tensor=UL.t, offset=g * 64 * ROWU, ap=[[128, A], [ROWU, 64], [1, 130]])
            P.dma("sp", Ur, Ur[:], UL, src)
            wb0, wb1, wb2, wb3 = [swp[0:A, g, k, :].unsqueeze(2).to_broadcast([A, 64, 128]) for k in range(4)]
            P.op("dve", lambda e: e.tensor_tensor(out=Z[:], in0=Ur[:, :, 0:128], in1=wb0, op=ALU.mult), reads=[Ur, swp], writes=[Z])
            P.op("pool", lambda e: e.tensor_tensor(out=Y[:], in0=Ur[:, :, 1:129], in1=wb1, op=ALU.mult), reads=[Ur, swp], writes=[Y])
            P.op("dve", lambda e: e.tensor_tensor(out=Z[:], in0=Z[:], in1=Y[:], op=ALU.add), reads=[Z, Y], writes=[Z])
            P.op("pool", lambda e: e.tensor_tensor(out=Y[:], in0=Ur[:, :, 2:130], in1=wb2, op=ALU.mult), reads=[Ur, swp], writes=[Y])
            P.op("dve", lambda e: e.tensor_tensor(out=Z[:], in0=Z[:], in1=Y[:], op=ALU.add), reads=[Z, Y], writes=[Z])
            P.op("dve", lambda e: e.tensor_tensor(out=Z[:], in0=Z[:], in1=wb3, op=ALU.add), reads=[Z, swp], writes=[Z])
            if g < 2:
                P.dma("sp", xg, xg[g], Z, Z[:])
        P.barrier()
        P.release(mU)
        ROWZ = n + 256
        nu = A + 1
        uh = (nu + 1) // 2
        Hm = P.sbuf("Hm", [128, 64, 2 * A], BF16)
        Zt = [[P.sbuf(f"Zt{b}{h}", [128, uh * 128], BF16) for h in range(2)] for b in range(2)]
        gts = [P.sbuf(f"gt{i}", [A, 8, 128], F32) for i in range(2)]
        for o in range(2):
            P.dma("sp", Hm, Hm[:], hmd, hmd[:, o, :, :])
            zdst = AP_(tensor=zd.t, offset=128, ap=[[128, A], [ROWZ, 64], [1, 128]])
            P.dma("pool", zd, zdst, Z, Z[:])
            for cl in range(64):
                zt = Zt[cl % 2]
                for h in range(2):
                    u0 = h * uh
                    nb = min(uh, nu - u0)
                    src = AP_(tensor=zd.t, offset=cl * ROWZ + 1 + u0 * 128, ap=[[1, 128], [1, nb * 128]])
                    P.dma("sp" if h == 0 else "act", zt[h], zt[h][:, 0:nb * 128], zd, src)
                if cl % 4 == 0:
                    psy = bank()
                yo = psy[0:A, (cl % 4) * 128:(cl % 4 + 1) * 128]
                for u in range(nu):
                    h = 0 if u < uh else 1
                    uu = u - h * uh
                    P.op("pe", lambda e: e.matmul(yo, Hm[:, cl, A - u:2 * A - u], zt[h][:, uu * 128:(uu + 1) * 128],
                                                  start=(u == 0), stop=(u == nu - 1)),
                         reads=[Hm, zt[h]], writes=[psy], inc=(u == nu - 1))
                if cl % 4 == 3:
                    P.op("act", lambda e: e.activation(out=Y[:, cl - 3:cl + 1, :], in_=psy[0:A, :].rearrange("p (c i) -> p c i", i=128), func=AF.Copy),
                         reads=[psy], writes=[Y])
            rb_ = rn[0:A, o * 64:(o + 1) * 64].unsqueeze(2).to_broadcast([A, 64, 128])
            bb_ = hbs[0:A, o, :].unsqueeze(2).to_broadcast([A, 64, 128])
            P.op("dve", lambda e: e.tensor_tensor(out=Y[:], in0=Y[:], in1=rb_, op=ALU.mult), reads=[Y, rn], writes=[Y])
            P.op("pool", lambda e: e.tensor_tensor(out=Z[:], in0=Z[:], in1=bb_, op=ALU.mult), reads=[Z, hbs], writes=[Z])
            P.op("dve", lambda e: e.tensor_tensor(out=Y[:], in0=Y[:], in1=Z[:], op=ALU.add), reads=[Y, Z], writes=[Y])
            for c4 in range(8):
                gt = gts[c4 % 2]
                P.dma("sp", gt, gt[:], xg, xg[o, :, c4 * 8:(c4 + 1) * 8, :])
                P.op("dve", lambda e: e.tensor_tensor(out=Z[:, c4 * 8:(c4 + 1) * 8, :], in0=Y[:, c4 * 8:(c4 + 1) * 8, :], in1=gt[:], op=ALU.mult),
                     reads=[Y, gt], writes=[Z])
        mdst = AP_(tensor=Mloc.t, offset=128 * NKEY + mcol0, ap=[[128, A], [NKEY, 64], [1, 128]])
        P.dma("sp", Mloc, mdst, Z, Z[:])
        P.barrier()
        P.release(mH)

    rn_main = [None]

    def phase_B3(l, do_ctx):
        if rn_main[0] is None:
            rn_main[0] = P.sbuf("rn_main", [128, 128], F32)
        rn = rn_main[0]
        hy_filters(l, SEQ // 128, embx, ntx, HMD, rn)
        hyena_stream(l, SEQ // 128, SEQ, ULp, XG, ZD, HMD, rn, 0, False)
        if do_ctx:
            hy_filters(l, CTX // 128, embc, ntxc, HMDc, rn)
            hyena_stream(l, CTX // 128, CTX, ULc, XGc, ZDc, HMDc, rn, SEQ, True)

    M2i = P.dram("M2i", [8 * 192, NKEY], F32); M2o = P.dram("M2o", [8 * 192, NKEY], F32)
    ML = P.dram("ML", [8 * 192, TOK], F32)
    XCUR = P.dram("XCUR", [D, TOK], F32); XC = P.dram("XC", [D, CTX], F32)
    outT = P.dram("outT", [D, TOK], F32, kind="ExternalOutput") if stage >= 5 else None
    if stage == 5:
        outs["dbg_ml"] = P.dram("dbg_ml", [8 * 192, TOK], F32, kind="ExternalOutput")
        outs["dbg_mctx"] = P.dram("dbg_mctx", [8 * 192, CTX], F32, kind="ExternalOutput")
        outs["dbg_xcur"] = P.dram("dbg_xcur", [D, TOK], F32, kind="ExternalOutput")
        outs["dbg_xc"] = P.dram("dbg_xc", [D, CTX], F32, kind="ExternalOutput")
        outs["dbg_co5"] = P.dram("dbg_co5", [512, TOK], BF16, kind="ExternalOutput")

    def exchange_M(l):
        P.dma("sp", M2i, M2i[:, :].rearrange("(j q r) t -> j q (r t)", j=8, q=64)[bass.ds(pid, 1), :, :].rearrange("a q f -> (a q) f"),
              Mloc, Mloc[:, :].rearrange("(q r) t -> q (r t)", q=64))
        allreduce(M2i, M2o, f"m2_{l}")
        src = M2o[:, 0:SEQ].rearrange("f (j t) -> j f t", j=8)[bass.ds(pid, 1), :, :].rearrange("a f t -> (a f) t")
        P.dma("sp", ML, ML[:, :], M2o, src)

    def phase_C(l, streams):
        mC = P.mark()
        ssum = banks[7]
        bi = [0]

        def bankC():
            b = banks[bi[0] % 7]
            bi[0] += 1
            return b
        mixH = P.sbuf("mixH", [128, KC, NT], BF16)
        X = P.sbuf("Xc", [128, KC, NT], F32)
        Yb = P.sbuf("Yb", [128, KC, NT], F32)
        Hf = P.sbuf("Hf", [128, 4 * KC, NT], BF16)
        wts = [P.sbuf(f"wtC{i}", [128, KC, 512], BF16) for i in range(2)]
        sqs = [P.sbuf(f"sqC{i}", [128, NT], BF16) for i in range(2)]
        rs = P.sbuf("rsC", [128, NT], F32)
        tmp = [P.sbuf(f"tmpC{i}", [128, NT], F32) for i in range(2)]
        wi = [0]

        def load_w(wn, kg, cg):
            wq_ = "sp" if wi[0] % 2 == 0 else "act"
            wt = wts[wi[0] % 2]; wi[0] += 1
            W = WG[(wn, l)]
            src = W[kg * 2048:(kg + 1) * 2048, cg * 512:(cg + 1) * 512].rearrange("(k p) n -> p k n", p=128)
            for kk in range(0, KC, 4):
                P.dma(wq_, wt, wt[:, kk:kk + 4, :], W, src[:, kk:kk + 4, :], disjoint=True)
            return wt

        def post_norm_residual(n, lv, kindG, dst_tile, dst_ap_fn):
            P.op("act", lambda e: e.activation(out=rs[:, 0:n], in_=ssum[:, 0:n], func=AF.Sqrt, scale=1.0 / D, bias=epsc[:, 0:1]),
                 reads=[ssum, epsc], writes=[rs])
            P.op("dve", lambda e: e.reciprocal(out=rs[:, 0:n], in_=rs[:, 0:n]), reads=[rs], writes=[rs])
            for k in range(KC):
                tk = tmp[k % 2]
                P.op("dve", lambda e: e.tensor_tensor(out=tk[:, 0:n], in0=Yb[:, k, 0:n], in1=rs[:, 0:n], op=ALU.mult), reads=[Yb, rs], writes=[tk])
                P.op("dve", lambda e: e.scalar_tensor_tensor(out=X[:, k, 0:n], in0=tk[:, 0:n], scalar=mc[:, lv, kindG, k:k + 1], in1=X[:, k, 0:n],
                                                            op0=ALU.mult, op1=ALU.add), reads=[tk, mc, X], writes=[X])
            if dst_tile is not None:
                for kk in range(0, KC, 4):
                    P.dma("sp", dst_tile, dst_ap_fn(kk), X, X[:, kk:kk + 4, 0:n], disjoint=True)

        def evac_group(pss, oc0, n, first, last):
            for j, ps in enumerate(pss):
                oc = oc0 + j
                P.op("act", lambda e: e.activation(out=Yb[:, oc, 0:n], in_=ps[:, 0:n], func=AF.Copy), reads=[ps], writes=[Yb])
                sq_ = sqs[oc % 2]
                P.op("act", lambda e: e.activation(out=sq_[:, 0:n], in_=ps[:, 0:n], func=AF.Square), reads=[ps], writes=[sq_])
                P.op("pe", lambda e: e.matmul(ssum[:, 0:n], ones_b[:], sq_[:, 0:n], start=(first and j == 0), stop=(last and j == len(pss) - 1)),
                     reads=[ones_b, sq_], writes=[ssum])

        for (isctx, t0, n, lv, xsrc, xdst) in streams:
            xs = xsrc[:, :].rearrange("(k p) t -> p k t", p=128)
            for kk in range(0, KC, 4):
                P.dma("sp", X, X[:, kk:kk + 4, 0:n], xsrc, xs[:, kk:kk + 4, t0:t0 + n], disjoint=True)
            co = COc if isctx else CO
            P.dma("sp", mixH, mixH[:, 0:4, 0:n], co, co[:, t0:t0 + n].rearrange("(c p) t -> p c t", p=128))
            for r in range(8):
                if isctx:
                    srcA = M2o[r * 192:r * 192 + 128, SEQ + t0:SEQ + t0 + n]
                    srcH = M2o[r * 192 + 128:r * 192 + 192, SEQ + t0:SEQ + t0 + n]
                    P.dma("pool", mixH, mixH[:, 4 + r, 0:n], M2o, srcA)
                    P.dma("pool", mixH, mixH[(r % 2) * 64:(r % 2) * 64 + 64, 12 + r // 2, 0:n], M2o, srcH)
                else:
                    P.dma("pool", mixH, mixH[:, 4 + r, 0:n], ML, ML[r * 192:r * 192 + 128, t0:t0 + n])
                    P.dma("pool", mixH, mixH[(r % 2) * 64:(r % 2) * 64 + 64, 12 + r // 2, 0:n], ML, ML[r * 192 + 128:r * 192 + 192, t0:t0 + n])
            for cg in range(4):
                wt = load_w("w_out", 0, cg)
                pss = [bankC() for _ in range(4)]
                for j, ps in enumerate(pss):
                    for k in range(KC):
                        P.op("pe", lambda e: e.matmul(ps[:, 0:n], wt[:, k, j * 128:(j + 1) * 128], mixH[:, k, 0:n], start=(k == 0), stop=(k == KC - 1)),
                             reads=[wt, mixH], writes=[ps], inc=(k == KC - 1))
                evac_group(pss, cg * 4, n, cg == 0, cg == 3)
            post_norm_residual(n, lv, 2, None, None)
            for k in range(KC):
                sq_ = sqs[k % 2]
                P.op("act", lambda e: e.activation(out=sq_[:, 0:n], in_=X[:, k, 0:n], func=AF.Square), reads=[X], writes=[sq_])
                P.op("pe", lambda e: e.matmul(ssum[:, 0:n], ones_b[:], sq_[:, 0:n], start=(k == 0), stop=(k == KC - 1)),
                     reads=[ones_b, sq_], writes=[ssum])
            P.op("act", lambda e: e.activation(out=rs[:, 0:n], in_=ssum[:, 0:n], func=AF.Sqrt, scale=1.0 / D, bias=epsc[:, 0:1]),
                 reads=[ssum, epsc], writes=[rs])
            P.op("dve", lambda e: e.reciprocal(out=rs[:, 0:n], in_=rs[:, 0:n]), reads=[rs], writes=[rs])
            for k in range(KC):
                tk = tmp[k % 2]
                P.op("dve", lambda e: e.tensor_tensor(out=tk[:, 0:n], in0=X[:, k, 0:n], in1=rs[:, 0:n], op=ALU.mult), reads=[X, rs], writes=[tk])
                P.op("act", lambda e: e.activation(out=mixH[:, k, 0:n], in_=tk[:, 0:n], func=AF.Identity,
                                                   scale=mc[:, lv, 3, k:k + 1], bias=mc[:, lv, 4, k:k + 1]), reads=[tk, mc], writes=[mixH])
            for cg in range(16):
                wt = load_w("w_ff1", 0, cg)
                pss = [bankC() for _ in range(4)]
                for j, ps in enumerate(pss):
                    for k in range(KC):
                        P.op("pe", lambda e: e.matmul(ps[:, 0:n], wt[:, k, j * 128:(j + 1) * 128], mixH[:, k, 0:n], start=(k == 0), stop=(k == KC - 1)),
                             reads=[wt, mixH], writes=[ps], inc=(k == KC - 1))
                for j, ps in enumerate(pss):
                    tk = tmp[j % 2]
                    P.op("act", lambda e: e.activation(out=tk[:, 0:n], in_=ps[:, 0:n], func=AF.Relu), reads=[ps], writes=[tk])
                    P.op("dve", lambda e: e.tensor_tensor(out=Hf[:, cg * 4 + j, 0:n], in0=tk[:, 0:n], in1=tk[:, 0:n], op=ALU.mult), reads=[tk], writes=[Hf])
            for cg in range(4):
                pss = [bankC() for _ in range(4)]
                for kg in range(4):
                    wt = load_w("w_ff2", kg, cg)
                    for j, ps in enumerate(pss):
                        for k in range(KC):
                            P.op("pe", lambda e: e.matmul(ps[:, 0:n], wt[:, k, j * 128:(j + 1) * 128], Hf[:, kg * KC + k, 0:n],
                                                          start=(kg == 0 and k == 0), stop=(kg == 3 and k == KC - 1)),
                                 reads=[wt, Hf], writes=[ps], inc=(kg == 3 and k == KC - 1) or (k == KC - 1))
                evac_group(pss, cg * 4, n, cg == 0, cg == 3)
            xd = xdst[:, :].rearrange("(k p) t -> p k t", p=128)
            post_norm_residual(n, lv, 5, xdst, lambda kk: xd[:, kk:kk + 4, t0:t0 + n])
        P.barrier()
        P.release(mC)

    phase_A(0, xT, True)

    if stage == 1:
        for r in range(0, E1_ROWS, 128):
            n = min(128, E1_ROWS - r)
            P.dma("sp", outs["dbg_e1"], outs["dbg_e1"][r:r + n, :], E1loc, E1loc[r:r + n, :])
        P.dma("sp", outs["dbg_yg"], outs["dbg_yg"][:, :], YG, YG[:, :])
        P.dma("sp", outs["dbg_pc"], outs["dbg_pc"][:, :], PC, PC[:, :])
        P.barrier()
        P.emit()
        return nc, list(outs.keys())

    allreduce(E1i, E1o, "e1_0")
    halo_exchange()
    if stage == 3:
        phase_B2(0, True)
        P.dma("sp", outs["dbg_co"], outs["dbg_co"][:, :], CO, CO[:, :])
        P.dma("sp", outs["dbg_coc"], outs["dbg_coc"][:, :], COc, COc[:, :])
        P.barrier()
        P.emit()
        return nc, list(outs.keys())
    if stage == 4:
        hy_init()
        mrn = P.mark()
        phase_B3(0, True)
        for r0 in range(0, NKEY, 2048):
            n = min(2048, NKEY - r0)
            P.dma("sp", outs["dbg_hy"], outs["dbg_hy"][:, r0:r0 + n], Mloc, Mloc[128:192, r0:r0 + n])
        P.barrier()
        P.emit()
        return nc, list(outs.keys())
    hy_init()
    zero_fill(M2i, 8 * 192 * NKEY, F32)
    for wn in ("w_out", "w_ff1", "w_ff2"):
        gather_weight(wn, 0)
    gather_weight("w_in", 1)
    phase_B1(0, True)
    phase_B2(0, True)
    phase_B3(0, True)
    exchange_M(0)
    for wn in ("w_out", "w_ff1", "w_ff2"):
        gather_weight(wn, 1)
    st0 = [(False, t0, NT, 0, xT, XCUR) for t0 in range(0, TOK, NT)] + [(True, 0, CTX, 1, ctxT, XC)]
    phase_C(0, st0)
    if stage == 5:
        for r0 in range(0, 1536, 128):
            P.dma("sp", outs["dbg_ml"], outs["dbg_ml"][r0:r0 + 128, :], ML, ML[r0:r0 + 128, :])
            P.dma("sp", outs["dbg_mctx"], outs["dbg_mctx"][r0:r0 + 128, :], M2o, M2o[r0:r0 + 128, SEQ:SEQ + CTX])
        for r0 in range(0, D, 128):
            P.dma("sp", outs["dbg_xcur"], outs["dbg_xcur"][r0:r0 + 128, :], XCUR, XCUR[r0:r0 + 128, :])
        P.dma("sp", outs["dbg_xc"], outs["dbg_xc"][:, :], XC, XC[:, :])
        P.dma("sp", outs["dbg_co5"], outs["dbg_co5"][:, :], CO, CO[:, :])
        P.barrier()
        P.emit()
        return nc, list(outs.keys())
    phase_A(1, XCUR, False)
    allreduce(E1i, E1o, "e1_1")
    halo_exchange(1)
    phase_B1(1, False)
    phase_B2(1, False)
    phase_B3(1, False)
    exchange_M(1)
    st1 = [(False, t0, NT, 2, XCUR, outT) for t0 in range(0, TOK, NT)]
    phase_C(1, st1)
    if stage == 2:
        for r0 in range(0, NKEY, 2048):
            n = min(2048, NKEY - r0)
            P.dma("sp", outs["dbg_attn"], outs["dbg_attn"][:, r0:r0 + n], Mloc, Mloc[0:128, r0:r0 + n])
        P.barrier()
        P.emit()
        return nc, list(outs.keys())

    P.barrier()
    P.emit()
    return nc, list(outs.keys())


def kernel(**inputs):
    cores = host_prep(inputs)
    nc, onames = build()
    res = run_bass_kernel_spmd(nc, cores, core_ids=list(range(NCORE)))
    outT = np.concatenate([res.results[j]["outT"] for j in range(NCORE)], axis=1)
    return np.ascontiguousarray(outT.T)[None].astype(np.float32)
```
